# Optimizing a Trainium2 kernel written in Bass

```python
import math
import jax
import jax.numpy as jnp
from jax import lax
import numpy as np


D_MODEL = 4096
BATCH = 8
SEQ = 2048
DEPTH = 2
DEC_BATCH = 1
DEC_SEQ = 8192
PAST_LEN = 128

EPS = 1e-6
NEG = -1e30
ROPE_THETA = 10000.0
MIX = D_MODEL
GROUP_W = (3 * MIX) // 8
HY_CH = MIX - 2 * GROUP_W
HY_EMB = 33
HY_BANDS = (HY_EMB - 1) // 2
HY_FFN = 64
HY_FAST_DECAY = 0.3
HY_SLOW_DECAY = 1.5
HY_TARGET = 1e-2
Q_LORA = 1536
KV_LORA = 512
NOPE_DIM = 128
ROPE_DIM = 64
V_DIM = 128
QK_DIM = NOPE_DIM + ROPE_DIM
MLA_HEADS = GROUP_W // V_DIM
ATT_BLOCK = 128
DIL_DIM = 128
DIL_PAIRS = ((128, 1), (512, 4), (2048, 16))
DIL_HEADS = GROUP_W // DIL_DIM
DIL_SLOTS = DIL_HEADS // len(DIL_PAIRS)
DIL_BLOCK = 64
D_FF = 4 * D_MODEL
HY_COLS = 3 * HY_CH
MLA_COLS = Q_LORA + KV_LORA + ROPE_DIM
DIL_COLS = 3 * DIL_HEADS * DIL_DIM
IN_COLS = HY_COLS + MLA_COLS + DIL_COLS

kernel_name = 'hybrid_hyena_mla_dilated_encoder'


def rmsnorm(x, g):
    xf = x.astype(jnp.float32)
    y = xf * lax.rsqrt(jnp.mean(xf * xf, axis=-1, keepdims=True) + EPS)
    return (y * g.astype(jnp.float32)).astype(x.dtype)


def rope_tables(L, dim):
    inv = 1.0 / (ROPE_THETA ** (jnp.arange(0, dim, 2, dtype=jnp.float32) / dim))
    ang = jnp.arange(L, dtype=jnp.float32)[:, None] * inv[None, :]
    return jnp.cos(ang), jnp.sin(ang)


def apply_rope(x, cos, sin):
    x1, x2 = jnp.split(x.astype(jnp.float32), 2, axis=-1)
    c = cos[None, :, None, :]
    s = sin[None, :, None, :]
    return jnp.concatenate([x1 * c - x2 * s, x2 * c + x1 * s], axis=-1).astype(x.dtype)


def hyena_filter(L, w1, b1, w2, b2, w3, b3, w4, b4, freq):
    t = jnp.linspace(0.0, 1.0, L, dtype=jnp.float32)[:, None]
    w = 2.0 * math.pi * jnp.arange(L, dtype=jnp.float32)[:, None] / L
    f = jnp.linspace(1e-4, HY_BANDS - 1, HY_BANDS, dtype=jnp.float32)[None, :]
    z = jnp.concatenate([t, jnp.cos(f * w), -jnp.sin(f * w)], axis=-1)
    fq = freq.astype(jnp.float32)
    h = jnp.sin(fq * (z @ w1.astype(jnp.float32) + b1.astype(jnp.float32)))
    h = jnp.sin(fq * (h @ w2.astype(jnp.float32) + b2.astype(jnp.float32)))
    h = jnp.sin(fq * (h @ w3.astype(jnp.float32) + b3.astype(jnp.float32)))
    h = h @ w4.astype(jnp.float32) + b4.astype(jnp.float32)
    max_decay = math.log(HY_TARGET) / HY_FAST_DECAY
    min_decay = math.log(HY_TARGET) / HY_SLOW_DECAY
    deltas = jnp.abs(jnp.linspace(min_decay, max_decay, HY_CH, dtype=jnp.float32))
    decay = jnp.exp(-t * deltas[None, :])
    return h * jnp.concatenate([decay, decay], axis=-1)


def hyena_mixer(z, conv_w, conv_b, w1, b1, w2, b2, w3, b3, w4, b4, freq, skip):
    B, L, _ = z.shape
    zp = jnp.pad(z, ((0, 0), (1, 1), (0, 0)))
    zc = zp[:, :-2] * conv_w[0] + zp[:, 1:-1] * conv_w[1] + zp[:, 2:] * conv_w[2] + conv_b
    x0, x1, v = jnp.split(zc, 3, axis=-1)
    u = (x1 * v).astype(jnp.float32)
    h = hyena_filter(L, w1, b1, w2, b2, w3, b3, w4, b4, freq)
    h_f, h_b = h[:, :HY_CH], h[:, HY_CH:]
    k_circ = jnp.concatenate([h_f, jnp.zeros((1, HY_CH), jnp.float32), h_b[:0:-1]], axis=0)
    uf = jnp.fft.rfft(u, n=2 * L, axis=1)
    kf = jnp.fft.rfft(k_circ, n=2 * L, axis=0)
    y = jnp.fft.irfft(uf * kf[None], n=2 * L, axis=1)[:, :L]
    y = y + u * skip.astype(jnp.float32)
    return x0 * y.astype(z.dtype)


def mla_mixer(cols, q_a_norm, w_q_b, kv_a_norm, w_kv_b, qn_nope, qn_rope, kn_nope, kn_rope):
    B, L, _ = cols.shape
    c_q = rmsnorm(cols[..., :Q_LORA], q_a_norm)
    c_kv = rmsnorm(cols[..., Q_LORA:Q_LORA + KV_LORA], kv_a_norm)
    k_rope = rmsnorm(cols[..., Q_LORA + KV_LORA:], kn_rope)[:, :, None, :]
    q = jnp.einsum('blr,rf->blf', c_q, w_q_b).reshape(B, L, MLA_HEADS, QK_DIM)
    kv = jnp.einsum('blr,rf->blf', c_kv, w_kv_b).reshape(B, L, MLA_HEADS, NOPE_DIM + V_DIM)
    q_nope = rmsnorm(q[..., :NOPE_DIM], qn_nope)
    q_rope = rmsnorm(q[..., NOPE_DIM:], qn_rope)
    k_nope = rmsnorm(kv[..., :NOPE_DIM], kn_nope)
    v = kv[..., NOPE_DIM:]
    cos, sin = rope_tables(L, ROPE_DIM)
    q_rope = apply_rope(q_rope, cos, sin)
    k_rope = apply_rope(k_rope, cos, sin)[:, :, 0, :]
    nq = L // ATT_BLOCK
    scale = QK_DIM ** -0.5

    def blocks(t):
        return jnp.moveaxis(t.reshape(B, nq, ATT_BLOCK, *t.shape[2:]), 1, 0)

    def one_block(args):
        qn, qr = args
        s = (jnp.einsum('bqhd,bkhd->bhqk', qn, k_nope, preferred_element_type=jnp.float32)
             + jnp.einsum('bqhd,bkd->bhqk', qr, k_rope, preferred_element_type=jnp.float32)) * scale
        p = jax.nn.softmax(s, axis=-1)
        return jnp.einsum('bhqk,bkhd->bqhd', p.astype(v.dtype), v)

    o = lax.map(one_block, (blocks(q_nope), blocks(q_rope)))
    return jnp.moveaxis(o, 0, 1).reshape(B, L, MLA_HEADS * V_DIM)


def dilated_branch(q, k, v, dil, radius):
    B, L, h, dh = q.shape
    n = L // dil
    nblk = -(-n // DIL_BLOCK)
    npad = nblk * DIL_BLOCK
    BD = B * dil

    def by_stride(t):
        t = t.reshape(B, n, dil, h, dh).transpose(0, 2, 1, 3, 4).reshape(BD, n, h, dh)
        return jnp.pad(t, ((0, 0), (0, npad - n), (0, 0), (0, 0)))

    def band(t):
        tp = jnp.pad(t, ((0, 0), (DIL_BLOCK, DIL_BLOCK), (0, 0), (0, 0))).reshape(BD, nblk + 2, DIL_BLOCK, h, dh)
        return jnp.concatenate([tp[:, :-2], tp[:, 1:-1], tp[:, 2:]], axis=2)

    qb = by_stride(q).reshape(BD, nblk, DIL_BLOCK, h, dh)
    kb = band(by_stride(k))
    vb = band(by_stride(v))
    s = jnp.einsum('bnqhd,bnkhd->bnhqk', qb, kb, preferred_element_type=jnp.float32) * (dh ** -0.5)
    qpos = jnp.arange(nblk)[:, None, None] * DIL_BLOCK + jnp.arange(DIL_BLOCK)[None, :, None]
    kpos = (jnp.arange(nblk)[:, None, None] - 1) * DIL_BLOCK + jnp.arange(3 * DIL_BLOCK)[None, None, :]
    valid = (jnp.abs(qpos - kpos) <= radius) & (kpos >= 0) & (kpos < n)
    s = jnp.where(valid[None, :, None], s, NEG)
    m = jnp.max(s, axis=-1, keepdims=True)
    p = jnp.exp(s - m)
    l = jnp.sum(p, axis=-1, keepdims=True)
    o = jnp.einsum('bnhqk,bnkhd->bnqhd', (p / l).astype(v.dtype), vb)
    lse = (m + jnp.log(l))[..., 0]
    o = o.reshape(B, dil, npad, h, dh)[:, :, :n].transpose(0, 2, 1, 3, 4).reshape(B, L, h, dh)
    lse = lse.transpose(0, 1, 3, 2).reshape(B, dil, npad, h)[:, :, :n].transpose(0, 2, 1, 3).reshape(B, L, h)
    return o, lse


def dilated_mixer(cols, q_norm, k_norm):
    B, L, _ = cols.shape
    q, k, v = [t.reshape(B, L, DIL_HEADS, DIL_DIM) for t in jnp.split(cols, 3, axis=-1)]
    q = rmsnorm(q, q_norm)
    k = rmsnorm(k, k_norm)
    cos, sin = rope_tables(L, DIL_DIM)
    q = apply_rope(q, cos, sin)
    k = apply_rope(k, cos, sin)
    outs, lses = [], []
    for g, (window, dil) in enumerate(DIL_PAIRS):
        sl = slice(g * DIL_SLOTS, (g + 1) * DIL_SLOTS)
        o, lse = dilated_branch(q[:, :, sl], k[:, :, sl], v[:, :, sl], dil, window // (2 * dil))
        outs.append(o)
        lses.append(lse)
    o = jnp.stack(outs, axis=2)
    alpha = jax.nn.softmax(jnp.stack(lses, axis=2), axis=2)
    o = o * alpha[..., None].astype(o.dtype)
    return o.reshape(B, L, DIL_HEADS * DIL_DIM)


def setup_inputs(seed: int = 0):
    key = jax.random.key(seed)
    ks = iter(jax.random.split(key, 40))

    def nrm(shape, scale):
        return jax.random.normal(next(ks), shape, jnp.float32) * scale

    def gain(shape):
        return 1.0 + 0.05 * jax.random.normal(next(ks), shape, jnp.float32)

    Ld = DEPTH
    return {
        'x_prompt': nrm((BATCH, SEQ, D_MODEL), 1.0),
        'x_sample': nrm((DEC_BATCH, DEC_SEQ, D_MODEL), 1.0),
        'norm_mix': gain((Ld, D_MODEL)),
        'w_in': nrm((Ld, D_MODEL, IN_COLS), D_MODEL ** -0.5),
        'hy_conv_w': nrm((Ld, 3, HY_COLS), 3 ** -0.5),
        'hy_conv_b': nrm((Ld, HY_COLS), 0.02),
        'hy_f_w1': nrm((Ld, HY_EMB, HY_FFN), HY_EMB ** -0.5),
        'hy_f_b1': nrm((Ld, HY_FFN), 0.02),
        'hy_f_w2': nrm((Ld, HY_FFN, HY_FFN), HY_FFN ** -0.5),
        'hy_f_b2': nrm((Ld, HY_FFN), 0.02),
        'hy_f_w3': nrm((Ld, HY_FFN, HY_FFN), HY_FFN ** -0.5),
        'hy_f_b3': nrm((Ld, HY_FFN), 0.02),
        'hy_f_w4': nrm((Ld, HY_FFN, 2 * HY_CH), HY_FFN ** -0.5),
        'hy_f_b4': nrm((Ld, 2 * HY_CH), 0.02),
        'hy_f_freq': gain((Ld, HY_FFN)),
        'hy_skip': nrm((Ld, HY_CH), 0.1),
        'mla_q_a_norm': gain((Ld, Q_LORA)),
        'mla_w_q_b': nrm((Ld, Q_LORA, MLA_HEADS * QK_DIM), Q_LORA ** -0.5),
        'mla_kv_a_norm': gain((Ld, KV_LORA)),
        'mla_w_kv_b': nrm((Ld, KV_LORA, MLA_HEADS * (NOPE_DIM + V_DIM)), KV_LORA ** -0.5),
        'mla_qn_nope': gain((Ld, NOPE_DIM)),
        'mla_qn_rope': gain((Ld, ROPE_DIM)),
        'mla_kn_nope': gain((Ld, NOPE_DIM)),
        'mla_kn_rope': gain((Ld, ROPE_DIM)),
        'dil_q_norm': gain((Ld, DIL_DIM)),
        'dil_k_norm': gain((Ld, DIL_DIM)),
        'out_norm': gain((Ld, MIX)),
        'w_out': nrm((Ld, MIX, D_MODEL), MIX ** -0.5),
        'norm_ffn': gain((Ld, D_MODEL)),
        'w_up': nrm((Ld, D_MODEL, D_FF), D_MODEL ** -0.5),
        'w_down': nrm((Ld, D_FF, D_MODEL), D_FF ** -0.5),
    }


def reference(x_prompt, x_sample, norm_mix, w_in, hy_conv_w, hy_conv_b, hy_f_w1, hy_f_b1, hy_f_w2, hy_f_b2,
              hy_f_w3, hy_f_b3, hy_f_w4, hy_f_b4, hy_f_freq, hy_skip, mla_q_a_norm, mla_w_q_b, mla_kv_a_norm,
              mla_w_kv_b, mla_qn_nope, mla_qn_rope, mla_kn_nope, mla_kn_rope, dil_q_norm, dil_k_norm,
              out_norm, w_out, norm_ffn, w_up, w_down):
    def trunk(x):
        for l in range(DEPTH):
            h = rmsnorm(x, norm_mix[l])
            z = jnp.einsum('bld,df->blf', h, w_in[l])
            y_hy = hyena_mixer(z[..., :HY_COLS], hy_conv_w[l], hy_conv_b[l], hy_f_w1[l], hy_f_b1[l],
                               hy_f_w2[l], hy_f_b2[l], hy_f_w3[l], hy_f_b3[l], hy_f_w4[l], hy_f_b4[l],
                               hy_f_freq[l], hy_skip[l])
            y_mla = mla_mixer(z[..., HY_COLS:HY_COLS + MLA_COLS], mla_q_a_norm[l], mla_w_q_b[l],
                              mla_kv_a_norm[l], mla_w_kv_b[l], mla_qn_nope[l], mla_qn_rope[l],
                              mla_kn_nope[l], mla_kn_rope[l])
            y_dil = dilated_mixer(z[..., HY_COLS + MLA_COLS:], dil_q_norm[l], dil_k_norm[l])
            g = out_norm[l]
            mixed = jnp.concatenate([rmsnorm(y_hy, g[:HY_CH]),
                                     rmsnorm(y_mla, g[HY_CH:HY_CH + GROUP_W]),
                                     rmsnorm(y_dil, g[HY_CH + GROUP_W:])], axis=-1)
            x = x + jnp.einsum('blm,md->bld', mixed, w_out[l])
            h = rmsnorm(x, norm_ffn[l])
            a = jax.nn.relu(jnp.einsum('bld,df->blf', h, w_up[l]))
            x = x + jnp.einsum('blf,fd->bld', a * a, w_down[l])
        return x

    y_prompt = trunk(x_prompt)
    y_sample = trunk(x_sample)
    return (y_prompt, y_sample)
```

```python
import math
from contextlib import ExitStack

import numpy as np
import ml_dtypes

import concourse.bass as bass
import concourse.mybir as mybir
from concourse.bass_utils import run_bass_kernel_spmd

F32 = mybir.dt.float32
BF16 = mybir.dt.bfloat16
AF = mybir.ActivationFunctionType
ALU = mybir.AluOpType
AX = mybir.AxisListType

D = 4096
KC = 32
DEPTH = 2
EPS = 1e-6
HY = 1024
GW = 1536
INC = 9792
DFF = 16384
NCORES = 8
LP = 2048
LS = 8192
SEM_LIMIT = 30000


class Buf:
    __slots__ = ("w", "r", "name", "excl")

    def __init__(self, name="", excl=False):
        self.w = None
        self.r = {}
        self.name = name
        self.excl = excl


class Prog:
    def __init__(self, nc, es):
        self.nc = nc
        self.es = es
        self.ops = []
        self.E = {"pe": nc.tensor, "act": nc.scalar, "dve": nc.vector, "pool": nc.gpsimd, "sp": nc.sync}
        self.last = {}
        self.open_dmas = []
        self.out_dmas = []
        self.nsem = 0

    def op(self, eng, name, reads, writes, **kw):
        return self.add(eng, (name, kw), reads, writes)

    def add(self, eng, fn, reads=(), writes=(), dma=False, extra=(), out=False):
        opid = len(self.ops)
        deps = set(extra)
        for b in reads:
            if b.w is not None:
                deps.add(b.w)
            if b.excl:
                deps.update(v for k, v in b.r.items() if k != eng)
        for b in writes:
            if b.w is not None:
                deps.add(b.w)
            deps.update(b.r.values())
        if eng == "pe" and not dma:
            deps = {d for d in deps if not (self.ops[d][0] == "pe" and not self.ops[d][3])}
        for d in deps:
            self.ops[d][4] = True
        self.ops.append([eng, fn, sorted(deps), dma, False])
        key = ("d", opid) if dma else eng
        for b in reads:
            b.r[key] = opid
        for b in writes:
            b.w = opid
            b.r = {}
        if dma:
            self.open_dmas.append(opid)
            if out:
                self.out_dmas.append(opid)
        elif fn is not None:
            self.last[eng] = opid
        return opid

    def dma(self, q, out_ap, in_ap, reads=(), writes=(), out=False, extra=()):
        return self.add(q, ("dma_start", dict(out=out_ap, in_=in_ap)), reads, writes, dma=True, out=out,
                        extra=extra)

    def opx(self, eng, name, reads, writes, extra, **kw):
        return self.add(eng, (name, kw), reads, writes, extra=extra)

    def barrier(self):
        deps = list(self.last.values()) + list(self.open_dmas)
        self.open_dmas = []
        for e in self.E:
            self.add(e, None, extra=deps)

    def _newsem(self, name):
        self.nsem += 1
        return self.es.enter_context(self.nc.semaphore(f"{name}_{self.nsem}"))

    def emit(self):
        E = self.E
        sig = {}
        engsem = {}
        known = {e: {} for e in E}
        semh = {}
        NDP = {"sp": 14, "pool": 8}
        dpool = {}
        for q, n in NDP.items():
            dpool[q] = [[self._newsem("d" + q), 0] for _ in range(n)]
        drot = {q: 0 for q in NDP}
        for opid, (eng, fn, deps, dma, signal) in enumerate(self.ops):
            En = E[eng]
            need = {}
            for d in deps:
                sk, val = sig[d]
                if need.get(sk, 0) < val:
                    need[sk] = val
            if dma:
                pl = dpool[eng]
                j = drot[eng] % len(pl)
                drot[eng] += 1
                if pl[j][1] + 16 > SEM_LIMIT:
                    pl[j] = [self._newsem("d" + eng), 0]
                sh, cnt = pl[j]
                sk = id(sh)
                semh[sk] = sh
                if cnt > 0:
                    need[sk] = max(need.get(sk, 0), cnt)
            for k2, val in need.items():
                if known[eng].get(k2, 0) < val:
                    En.wait_ge(semh[k2], val)
                    known[eng][k2] = val
            if fn is None:
                continue
            ins = getattr(En, fn[0])(**fn[1])
            if dma:
                ins.then_inc(sh, 16)
                pl[j][1] = cnt + 16
                sig[opid] = (sk, cnt + 16)
            elif signal:
                cur = engsem.get(eng)
                if cur is None or cur[1] + 1 > SEM_LIMIT:
                    cur = [self._newsem("e" + eng), 0]
                    engsem[eng] = cur
                    semh[id(cur[0])] = cur[0]
                cur[1] += 1
                ins.then_inc(cur[0], 1)
                sig[opid] = (id(cur[0]), cur[1])
        for d in self.out_dmas:
            sk, val = sig[d]
            if known["sp"].get(sk, 0) < val:
                self.nc.sync.wait_ge(semh[sk], val)
                known["sp"][sk] = val


class T:
    __slots__ = ("t", "b")

    def __init__(self, t, name, excl=False):
        self.t = t
        self.b = Buf(name, excl)

    def __getitem__(self, k):
        return self.t[k]


class Rot:
    def __init__(self, items):
        self.items = items
        self.i = 0

    def next(self):
        x = self.items[self.i % len(self.items)]
        self.i += 1
        return x


class Builder:
    def __init__(self, taps=(), stop=None, seqs=("p", "s"), depth=DEPTH):
        self.taps = set(taps)
        self.stop = stop
        self.seq_names = seqs
        self.depth = depth
        self.nc = bass.Bass("TRN2", target_bir_lowering=False)
        self.es = ExitStack()
        self.P = Prog(self.nc, self.es)
        self.dram = {}
        self.dbuf = {}

    def din(self, name, shape, dt=F32):
        a = self.nc.dram_tensor(name, list(shape), dt, kind="ExternalInput").ap()
        self.dram[name] = a
        self.dbuf[name] = Buf(name)
        return a

    def dout(self, name, shape, dt=F32):
        a = self.nc.dram_tensor(name, list(shape), dt, kind="ExternalOutput").ap()
        self.dram[name] = a
        self.dbuf[name] = Buf(name)
        return a

    def dscr(self, name, shape, dt=BF16):
        kind = "ExternalOutput" if name in self.taps else "Internal"
        a = self.nc.dram_tensor(name, list(shape), dt, kind=kind).ap()
        self.dram[name] = a
        self.dbuf[name] = Buf(name)
        return a

    _uid = 0

    def sb(self, es, name, shape, dt):
        Builder._uid += 1
        name = f"{name}_u{Builder._uid}"
        return T(es.enter_context(self.nc.sbuf_tensor(name, list(shape), dt)), name)

    def ps(self, es, name, shape, dt=F32):
        Builder._uid += 1
        name = f"{name}_u{Builder._uid}"
        return T(es.enter_context(self.nc.psum_tensor(name, list(shape), dt)), name, excl=True)


    PCOL = dict(gmix=0, gffn=32, gout=64, gqa=96, gkva=108, gknr=112, gqnn=113, gqnr=114, gknn=115,
                gdq=116, gdk=117, cw=118, cbias=190, skip=214, b1=222, b2=223, b3=224, freq=225)
    NPCOL = 226

    def declare(self):
        nd = self.depth
        self.xin = {"p": self.din("xp", [LP, D]), "s": self.din("xs", [LS, D])}
        self.yout = {"p": self.dout("yp", [LP, D]), "s": self.dout("ys", [LS, D])}
        self.w_in = self.din("w_in", [nd, D, INC])
        self.w_out = self.din("w_out", [nd, D, D])
        self.w_up = self.din("w_up", [nd, D, DFF])
        self.w_down = self.din("w_down", [nd, DFF, D])
        self.w_qb = self.din("w_qb", [nd, GW, 2304])
        self.w_kvb = self.din("w_kvb", [nd, 512, 3072])
        self.parm = self.din("parm", [nd, 128, self.NPCOL])
        self.fw1 = self.din("fw1", [nd, 33, 64])
        self.fw2 = self.din("fw2", [nd, 64, 64])
        self.fw3 = self.din("fw3", [nd, 64, 64])
        self.fw4 = self.din("fw4", [nd, 64, 2048])
        self.fb4 = self.din("fb4", [nd, 1, 2048])
        self.c_ident = self.din("c_ident", [128, 128], BF16)
        self.c_ones = self.din("c_ones", [128, 128], BF16)
        self.c_r64 = self.din("c_r64", [64, 64], BF16)
        self.c_r128 = self.din("c_r128", [128, 128], BF16)
        self.c_mask = self.din("c_mask", [128, 3, 128], BF16)
        self.cL = {}
        for s in self.seq_names:
            L = LP if s == "p" else LS
            nb = L // 128
            self.cL[s] = dict(
                L=L,
                cosm=self.din(f"cosm_{s}", [64, L]), sinm=self.din(f"sinm_{s}", [64, L]),
                cosd=self.din(f"cosd_{s}", [128, L]), sind=self.din(f"sind_{s}", [128, L]),
                zfT=self.din(f"zfT_{s}", [33, L]),
                decf=self.din(f"decf_{s}", [L, HY]), decb=self.din(f"decb_{s}", [L, HY]),
                Cf=self.din(f"Cf_{s}", [nb, 128, nb, 128], BF16), Sf=self.din(f"Sf_{s}", [nb, 128, nb, 128], BF16),
                Ci=self.din(f"Ci_{s}", [L // 256, 128, nb, 256], BF16),
                Si=self.din(f"Si_{s}", [L // 256, 128, nb, 256], BF16),
            )
        self.winb = self.dscr("winb", [nd, 65, 128, KC, 128])
        self.winv = self.dscr("winv", [nd, 3, 128, KC, 512])
        self.winkr = self.dscr("winkr", [nd, 128, KC, 64])
        self.woutb = self.dscr("woutb", [nd, D, D])
        self.wupb = self.dscr("wupb", [nd, 128, 128, KC, 128])
        self.wdnb = self.dscr("wdnb", [nd, DFF, D])
        self.wqbb = self.dscr("wqbb", [nd, GW, 2304])
        self.wkvbb = self.dscr("wkvbb", [nd, 512, 3072])
        self.S = {}
        for s in self.seq_names:
            L = self.cL[s]["L"]
            nb = L // 128
            sc = {}
            for nm, shp, dt in [
                ("zhyT", [3072, L], BF16), ("cqnT", [GW, L], BF16), ("ckvnT", [512, L], BF16), ("krT", [64, L], BF16),
                ("dqT", [GW, L], BF16), ("dkT", [GW, L], BF16), ("dv", [L, GW], BF16),
                ("qnT", [GW, L], BF16), ("qrT", [768, L], BF16), ("knT", [GW, L], BF16), ("vm", [L, GW], BF16),
                ("x0cT", [HY, L], BF16), ("uT", [HY, L], BF16), ("utm", [L, HY], BF16),
                ("Atm", [L, HY], BF16), ("Btm", [L, HY], BF16),
                ("Yrb", [8, 128, nb, 128], BF16), ("Yib", [8, 128, nb, 128], BF16),
                ("ymixT", [D, L], BF16), ("xmid", [L, D], F32),
            ]:
                sc[nm] = self.dscr(f"{nm}_{s}", shp, dt)
            self.S[s] = sc

    WIN_BLOCKS = [(128 * j, 128) for j in range(40)] + [(5120, 64)] + [(5184 + 128 * j, 128) for j in range(24)]

    def cast_weights(self):
        P = self.P
        for l in range(self.depth):
            for j, (off, w) in enumerate(self.WIN_BLOCKS):
                P.dma("pool", self.winb[l, j] if w == 128 else self.winkr[l],
                      self.w_in[l, :, off:off + w].rearrange("(kc p) m -> p kc m", p=128))
            for s3 in range(3):
                off = 8256 + 512 * s3
                P.dma("pool", self.winv[l, s3], self.w_in[l, :, off:off + 512].rearrange("(kc p) m -> p kc m", p=128))
            for r in range(8):
                P.dma("pool", self.woutb[l, 512 * r:512 * (r + 1), :], self.w_out[l, 512 * r:512 * (r + 1), :])
            for j in range(128):
                P.dma("pool", self.wupb[l, j],
                      self.w_up[l, :, 128 * j:128 * (j + 1)].rearrange("(kc p) m -> p kc m", p=128))
            for r in range(32):
                P.dma("pool", self.wdnb[l, 512 * r:512 * (r + 1), :], self.w_down[l, 512 * r:512 * (r + 1), :])
            for r in range(3):
                P.dma("pool", self.wqbb[l, 512 * r:512 * (r + 1), :], self.w_qb[l, 512 * r:512 * (r + 1), :])
            P.dma("pool", self.wkvbb[l], self.w_kvb[l])
        P.barrier()

    def load_consts(self):
        es, P = self.es, self.P
        self.ident = self.sb(es, "ident", [128, 128], BF16)
        self.ones = self.sb(es, "ones", [128, 128], BF16)
        self.r64 = self.sb(es, "r64", [64, 64], BF16)
        self.r128 = self.sb(es, "r128", [128, 128], BF16)
        self.mask = self.sb(es, "mask", [128, 3, 128], BF16)
        self.pm = [self.sb(es, f"pm{l}", [128, self.NPCOL], F32) for l in range(self.depth)]
        for t, src in [(self.ident, self.c_ident), (self.ones, self.c_ones), (self.r64, self.c_r64),
                       (self.r128, self.c_r128), (self.mask, self.c_mask)]:
            P.dma("sp", t[:], src, [], [t.b])
        for l in range(self.depth):
            P.dma("sp", self.pm[l][:], self.parm[l], [], [self.pm[l].b])

    def pc(self, l, name, i=0, n=1, rows=128):
        c = self.PCOL[name] + i
        return self.pm[l][0:rows, c:c + n]

    def rsq(self, ss, r, npart, n, width):
        P = self.P
        P.op("dve", "tensor_scalar", [ss.b], [r.b], out=r[0:npart, 0:n], in0=ss[0:npart, 0:n],
             scalar1=1.0 / width, scalar2=EPS, op0=ALU.mult, op1=ALU.add)
        P.op("act", "activation", [r.b], [r.b], out=r[0:npart, 0:n], in_=r[0:npart, 0:n], func=AF.Sqrt)
        P.op("dve", "reciprocal", [r.b], [r.b], out=r[0:npart, 0:n], in_=r[0:npart, 0:n])

    def headnorm(self, zt, zap, npart, n, gcol, pool, outt, outap, rope=None):
        P = self.P
        sq = pool["sq"].next()
        P.op("act", "activation", [zt.b], [sq.b], out=sq[0:npart, 0:n], in_=zap, func=AF.Square)
        ss = pool["ss"].next()
        P.op("pe", "matmul", [sq.b, self.ones.b], [ss.b], out=ss[0:npart, 0:n], lhsT=self.ones[0:npart, 0:npart],
             rhs=sq[0:npart, 0:n], start=True, stop=True)
        r = pool["r"].next()
        self.rsq(ss, r, npart, n, float(npart))
        if rope is None:
            P.op("dve", "scalar_tensor_tensor", [zt.b, r.b], [outt.b], out=outap, in0=zap, scalar=gcol,
                 in1=r[0:npart, 0:n], op0=ALU.mult, op1=ALU.mult)
            return
        RT, cos, sin = rope
        zn = pool["zn"].next()
        P.op("dve", "scalar_tensor_tensor", [zt.b, r.b], [zn.b], out=zn[0:npart, 0:n], in0=zap, scalar=gcol,
             in1=r[0:npart, 0:n], op0=ALU.mult, op1=ALU.mult)
        rp = pool["rp"].next()
        P.op("pe", "matmul", [zn.b, RT.b], [rp.b], out=rp[0:npart, 0:n], lhsT=RT[0:npart, 0:npart],
             rhs=zn[0:npart, 0:n], start=True, stop=True)
        t1 = pool["t1"].next()
        P.op("pool", "tensor_tensor", [zn.b, cos.b], [t1.b], out=t1[0:npart, 0:n], in0=zn[0:npart, 0:n],
             in1=cos[0:npart, 0:n], op=ALU.mult)
        P.op("dve", "tensor_tensor", [rp.b, sin.b], [r.b], out=r[0:npart, 0:n], in0=rp[0:npart, 0:n],
             in1=sin[0:npart, 0:n], op=ALU.mult)
        P.op("dve", "tensor_tensor", [r.b, t1.b], [outt.b], out=outap, in0=r[0:npart, 0:n], in1=t1[0:npart, 0:n],
             op=ALU.add)

    def norm_transpose(self, xt, gname, l, sub, hT, pool):
        P = self.P
        xn = pool["xn"].next()
        ssq = pool["ssq"].next()
        P.op("act", "activation", [xt.b], [xn.b, ssq.b], out=xn[:, :], in_=xt[:, :], func=AF.Square,
             accum_out=ssq[:, 0:1])
        P.op("dve", "tensor_scalar", [ssq.b], [ssq.b], out=ssq[:, 1:2], in0=ssq[:, 0:1], scalar1=1.0 / D,
             scalar2=EPS, op0=ALU.mult, op1=ALU.add)
        P.op("act", "activation", [ssq.b], [ssq.b], out=ssq[:, 3:4], in_=ssq[:, 1:2], func=AF.Sqrt)
        P.op("dve", "reciprocal", [ssq.b], [ssq.b], out=ssq[:, 2:3], in_=ssq[:, 3:4])
        P.op("dve", "tensor_scalar", [xt.b, ssq.b, xn.b], [xn.b], out=xn[:, :], in0=xt[:, :], scalar1=ssq[:, 2:3],
             scalar2=None, op0=ALU.mult)
        for g in range(4):
            tp = pool["tp"].next()
            for j in range(8):
                kc = g * 8 + j
                P.op("pe", "transpose", [xn.b, self.ident.b], [tp.b], out=tp[:, j, :],
                     in_=xn[:, kc * 128:(kc + 1) * 128], identity=self.ident[:, :])
            for j in range(8):
                kc = g * 8 + j
                if j % 2 == 0:
                    P.op("act", "activation", [tp.b], [hT.b], out=hT[:, kc, sub * 128:(sub + 1) * 128],
                         in_=tp[:, j, :], func=AF.Copy, scale=self.pc(l, gname, kc))
                else:
                    P.op("dve", "tensor_scalar", [tp.b], [hT.b], out=hT[:, kc, sub * 128:(sub + 1) * 128],
                         in0=tp[:, j, :], scalar1=self.pc(l, gname, kc), scalar2=None, op0=ALU.mult)

    def phase_A1(self, s, l, xsrc):
        P, c, sc = self.P, self.cL[s], self.S[s]
        L = c["L"]
        with ExitStack() as es:
            xin = Rot([self.sb(es, f"xin{i}", [128, D], F32) for i in range(2)])
            pool = dict(
                xn=Rot([self.sb(es, f"xn{i}", [128, D], BF16) for i in range(1)]),
                ssq=Rot([self.sb(es, f"ssq{i}", [128, 4], F32) for i in range(2)]),
                tp=Rot([self.ps(es, f"tp{i}", [128, 8, 128], BF16) for i in range(2)]),
                sq=Rot([self.sb(es, f"sq{i}", [128, 512], BF16) for i in range(2)]),
                ss=Rot([self.ps(es, f"ss{i}", [128, 512]) for i in range(2)]),
                r=Rot([self.sb(es, f"r{i}", [128, 512], F32) for i in range(2)]),
                zn=Rot([self.sb(es, f"zn{i}", [128, 512], BF16) for i in range(2)]),
                rp=Rot([self.ps(es, "rp0", [128, 512])]),
                t1=Rot([self.sb(es, f"t1{i}", [128, 512], F32) for i in range(2)]),
            )
            hTs = Rot([self.sb(es, f"hT{i}", [128, KC, 512], BF16) for i in range(1)])
            wsl = Rot([self.sb(es, f"wsl{i}", [128, KC, 128], BF16) for i in range(3)])
            wvs = Rot([self.sb(es, f"wvs{i}", [128, KC, 512], BF16) for i in range(1)])
            wkr = self.sb(es, "wkr", [128, KC, 64], BF16)
            zb = Rot([self.ps(es, f"zb{i}", [128, 512]) for i in range(3)])
            stg = Rot([self.sb(es, f"stg{i}", [128, 512], BF16) for i in range(4)])
            zq = self.sb(es, "zq", [128, 12, 512], BF16)
            ssacc = pool["ss"]
            cosm = Rot([self.sb(es, f"cosm{i}", [64, 512], F32) for i in range(2)])
            sinm = Rot([self.sb(es, f"sinm{i}", [64, 512], F32) for i in range(2)])
            cosd = Rot([self.sb(es, f"cosd{i}", [128, 512], F32) for i in range(2)])
            sind = Rot([self.sb(es, f"sind{i}", [128, 512], F32) for i in range(2)])
            for ti in range(L // 512):
                t0 = ti * 512
                tsl = slice(t0, t0 + 512)
                hT = hTs.next()
                for sub in range(4):
                    xt = xin.next()
                    P.dma("sp", xt[:, :], xsrc[t0 + sub * 128:t0 + (sub + 1) * 128, :], [], [xt.b])
                    self.norm_transpose(xt, "gmix", l, sub, hT, pool)
                cm, sm, cd, sd = cosm.next(), sinm.next(), cosd.next(), sind.next()
                P.dma("sp", cm[:, :], c["cosm"][:, tsl], [], [cm.b])
                P.dma("sp", sm[:, :], c["sinm"][:, tsl], [], [sm.b])
                P.dma("sp", cd[:, :], c["cosd"][:, tsl], [], [cd.b])
                P.dma("sp", sd[:, :], c["sind"][:, tsl], [], [sd.b])
                gstate = {}

                def post(j, z):
                    if j < 24:
                        o = stg.next()
                        P.op("act", "activation", [z.b], [o.b], out=o[:, :], in_=z[:, :], func=AF.Copy)
                        P.dma("pool", sc["zhyT"][j * 128:(j + 1) * 128, tsl], o[:, :], [o.b], [])
                    elif j < 40:
                        first, nblk, gname, dst = (24, 12, "gqa", "cqnT") if j < 36 else (36, 4, "gkva", "ckvnT")
                        i = j - first
                        P.op("dve", "tensor_copy", [z.b], [zq.b], out=zq[:, i, :], in_=z[:, :])
                        sq = pool["sq"].next()
                        P.op("act", "activation", [z.b], [sq.b], out=sq[:, :], in_=z[:, :], func=AF.Square)
                        if i == 0:
                            gstate["ssg"] = ssacc.next()
                        ssg = gstate["ssg"]
                        P.op("pe", "matmul", [sq.b, self.ones.b], [ssg.b], out=ssg[:, :], lhsT=self.ones[:, :],
                             rhs=sq[:, :], start=(i == 0), stop=(i == nblk - 1))
                        if i == nblk - 1:
                            r = pool["r"].next()
                            self.rsq(ssg, r, 128, 512, float(nblk * 128))
                            for i2 in range(nblk):
                                o = stg.next()
                                P.op("dve", "scalar_tensor_tensor", [zq.b, r.b], [o.b], out=o[:, :], in0=zq[:, i2, :],
                                     scalar=self.pc(l, gname, i2), in1=r[:, :], op0=ALU.mult, op1=ALU.mult)
                                P.dma("pool", sc[dst][i2 * 128:(i2 + 1) * 128, tsl], o[:, :], [o.b], [])
                    elif j == 40:
                        o = stg.next()
                        self.headnorm(z, z[0:64, :], 64, 512, self.pc(l, "gknr", rows=64), pool, o, o[0:64, :],
                                      rope=(self.r64, cm, sm))
                        P.dma("pool", sc["krT"][:, tsl], o[0:64, :], [o.b], [])
                    else:
                        hh = (j - 41) % 12
                        isq = j < 53
                        o = stg.next()
                        self.headnorm(z, z[:, :], 128, 512, self.pc(l, "gdq" if isq else "gdk"), pool, o, o[:, :],
                                      rope=(self.r128, cd, sd))
                        P.dma("pool", sc["dqT" if isq else "dkT"][hh * 128:(hh + 1) * 128, tsl], o[:, :], [o.b], [])

                prev = None
                for j, (off, w) in enumerate(self.WIN_BLOCKS):
                    if w == 128:
                        ws = wsl.next()
                        P.dma("sp", ws[:, :, :], self.winb[l, j], [], [ws.b])
                    else:
                        ws = wkr
                        P.dma("sp", ws[:, :, :], self.winkr[l], [], [ws.b])
                    z = zb.next()
                    for kc in range(KC):
                        P.op("pe", "matmul", [ws.b, hT.b], [z.b], out=z[0:w, :], lhsT=ws[:, kc, 0:w], rhs=hT[:, kc, :],
                             start=(kc == 0), stop=(kc == KC - 1))
                    if prev is not None:
                        post(*prev)
                    prev = (j, z)
                post(*prev)
                for s3 in range(3):
                    wv = wvs.next()
                    P.dma("sp", wv[:, :, :], self.winv[l, s3], [], [wv.b])
                    for sub in range(4):
                        z = zb.next()
                        for kc in range(KC):
                            P.op("pe", "matmul", [wv.b, hT.b], [z.b], out=z[:, :], lhsT=hT[:, kc, sub * 128:(sub + 1) * 128],
                                 rhs=wv[:, kc, :], start=(kc == 0), stop=(kc == KC - 1))
                        o = stg.next()
                        P.op("act", "activation", [z.b], [o.b], out=o[:, :], in_=z[:, :], func=AF.Copy)
                        P.dma("pool", sc["dv"][t0 + sub * 128:t0 + (sub + 1) * 128, s3 * 512:(s3 + 1) * 512], o[:, :],
                              [o.b], [])
        P.barrier()

    def phase_A2(self, s, l):
        P, c, sc = self.P, self.cL[s], self.S[s]
        L = c["L"]
        with ExitStack() as es:
            pool = dict(
                sq=Rot([self.sb(es, f"sq{i}", [128, 512], BF16) for i in range(2)]),
                ss=Rot([self.ps(es, f"ss{i}", [128, 512]) for i in range(2)]),
                r=Rot([self.sb(es, f"r{i}", [128, 512], F32) for i in range(2)]),
                zn=Rot([self.sb(es, f"zn{i}", [128, 512], BF16) for i in range(2)]),
                rp=Rot([self.ps(es, "rp0", [128, 512])]),
                t1=Rot([self.sb(es, f"t1{i}", [128, 512], F32) for i in range(2)]),
            )
            wq = self.sb(es, "wq", [128, 12, 2304], BF16)
            wkv = self.sb(es, "wkv", [128, 4, 3072], BF16)
            P.dma("sp", wq[:, :, :], self.wqbb[l].rearrange("(kc p) m -> p kc m", p=128), [], [wq.b])
            P.dma("sp", wkv[:, :, :], self.wkvbb[l].rearrange("(kc p) m -> p kc m", p=128), [], [wkv.b])
            cqs = Rot([self.sb(es, f"cq{i}", [128, 12, 512], BF16) for i in range(2)])
            cks = Rot([self.sb(es, f"ck{i}", [128, 4, 512], BF16) for i in range(2)])
            zb = Rot([self.ps(es, f"zb{i}", [128, 512]) for i in range(4)])
            stg = Rot([self.sb(es, f"stg{i}", [128, 512], BF16) for i in range(4)])
            cosm = Rot([self.sb(es, f"cosm{i}", [64, 512], F32) for i in range(2)])
            sinm = Rot([self.sb(es, f"sinm{i}", [64, 512], F32) for i in range(2)])
            for ti in range(L // 512):
                t0 = ti * 512
                tsl = slice(t0, t0 + 512)
                cq, ck, cm, sm = cqs.next(), cks.next(), cosm.next(), sinm.next()
                P.dma("sp", cq[:, :, :], sc["cqnT"][:, tsl].rearrange("(kc p) t -> p kc t", p=128), [], [cq.b])
                P.dma("sp", ck[:, :, :], sc["ckvnT"][:, tsl].rearrange("(kc p) t -> p kc t", p=128), [], [ck.b])
                P.dma("sp", cm[:, :], c["cosm"][:, tsl], [], [cm.b])
                P.dma("sp", sm[:, :], c["sinm"][:, tsl], [], [sm.b])
                for h in range(12):
                    z = zb.next()
                    for kc in range(12):
                        P.op("pe", "matmul", [wq.b, cq.b], [z.b], out=z[:, :], lhsT=wq[:, kc, h * 192:h * 192 + 128],
                             rhs=cq[:, kc, :], start=(kc == 0), stop=(kc == 11))
                    o = stg.next()
                    self.headnorm(z, z[:, :], 128, 512, self.pc(l, "gqnn"), pool, o, o[:, :])
                    P.dma("pool", sc["qnT"][h * 128:(h + 1) * 128, tsl], o[:, :], [o.b], [])
                    z = zb.next()
                    for kc in range(12):
                        P.op("pe", "matmul", [wq.b, cq.b], [z.b], out=z[0:64, :],
                             lhsT=wq[:, kc, h * 192 + 128:h * 192 + 192], rhs=cq[:, kc, :], start=(kc == 0), stop=(kc == 11))
                    o = stg.next()
                    self.headnorm(z, z[0:64, :], 64, 512, self.pc(l, "gqnr", rows=64), pool, o, o[0:64, :],
                                  rope=(self.r64, cm, sm))
                    P.dma("pool", sc["qrT"][h * 64:(h + 1) * 64, tsl], o[0:64, :], [o.b], [])
                    z = zb.next()
                    for kc in range(4):
                        P.op("pe", "matmul", [wkv.b, ck.b], [z.b], out=z[:, :], lhsT=wkv[:, kc, h * 256:h * 256 + 128],
                             rhs=ck[:, kc, :], start=(kc == 0), stop=(kc == 3))
                    o = stg.next()
                    self.headnorm(z, z[:, :], 128, 512, self.pc(l, "gknn"), pool, o, o[:, :])
                    P.dma("pool", sc["knT"][h * 128:(h + 1) * 128, tsl], o[:, :], [o.b], [])
                wkv4 = wkv.t.rearrange("p k (h c) -> p k h c", c=256)
                for sub in range(4):
                    for hg in range(3):
                        z = zb.next()
                        for kc in range(4):
                            P.op("pe", "matmul", [wkv.b, ck.b], [z.b], out=z[:, :].rearrange("p (h c) -> p h c", c=128),
                                 lhsT=ck[:, kc, sub * 128:(sub + 1) * 128], rhs=wkv4[:, kc, hg * 4:hg * 4 + 4, 128:256],
                                 start=(kc == 0), stop=(kc == 3))
                        o = stg.next()
                        P.op("act", "activation", [z.b], [o.b], out=o[:, :], in_=z[:, :], func=AF.Copy)
                        P.dma("pool", sc["vm"][t0 + sub * 128:t0 + (sub + 1) * 128, hg * 512:(hg + 1) * 512], o[:, :],
                              [o.b], [])
        P.barrier()

    def phase_C(self, s, l, xsrc, xdst, final):
        P, c, sc = self.P, self.cL[s], self.S[s]
        L = c["L"]
        with ExitStack() as es:
            bigA = self.sb(es, "bigA", [128, KC, 512], BF16)
            xacc = [self.sb(es, f"xacc{i}", [128, D], F32) for i in range(4)]
            WA = self.sb(es, "WA", [128, 16384], BF16)
            WB = self.sb(es, "WB", [128, 16384], BF16)
            wo_slab = [T(WA.t[:, :].rearrange("p (k c) -> p k c", c=512), "woA"),
                       T(WB.t[:, :].rearrange("p (k c) -> p k c", c=512), "woB")]
            a2T = T(WA.t[:, 0:8192].rearrange("p (k c) -> p k c", c=512), "a2T")
            wup = [T(WA.t[:, 8192 + 4096 * i:8192 + 4096 * (i + 1)].rearrange("p (k c) -> p k c", c=128), f"wup{i}")
                   for i in range(2)]
            wdn = [T(WB.t[:, 8192 * i:8192 * (i + 1)].rearrange("p (k c) -> p k c", c=512), f"wdn{i}")
                   for i in range(2)]
            pool = dict(
                xn=Rot([self.sb(es, "xnb", [128, D], BF16)]),
                ssq=Rot([self.sb(es, f"ssq{i}", [128, 4], F32) for i in range(2)]),
                tp=Rot([self.ps(es, f"tp{i}", [128, 8, 128], BF16) for i in range(2)]),
            )
            sqs = Rot([self.sb(es, f"sq{i}", [128, 512], BF16) for i in range(2)])
            ss = self.ps(es, "ss", [128, 512])
            rg = [self.sb(es, f"rg{i}", [128, 512], F32) for i in range(3)]
            rl = Rot([self.sb(es, f"rl{i}", [128, 512], F32) for i in range(2)])
            acc = Rot([self.ps(es, f"acc{i}", [128, 512]) for i in range(4)])
            ffn_last_pe = None
            for ti in range(L // 512):
                t0 = ti * 512
                tsl = slice(t0, t0 + 512)
                P.dma("sp", bigA[:, :, :], sc["ymixT"][:, tsl].rearrange("(kc p) t -> p kc t", p=128), [], [bigA.b])
                for sub in range(4):
                    P.dma("sp", xacc[sub][:, :], xsrc[t0 + sub * 128:t0 + (sub + 1) * 128, :], [], [xacc[sub].b])
                for gi, (k0, k1) in enumerate(((0, 8), (8, 20), (20, 32))):
                    for kc in range(k0, k1):
                        sq = sqs.next()
                        P.op("act", "activation", [bigA.b], [sq.b], out=sq[:, :], in_=bigA[:, kc, :], func=AF.Square)
                        P.op("pe", "matmul", [sq.b, self.ones.b], [ss.b], out=ss[:, :], lhsT=self.ones[:, :],
                             rhs=sq[:, :], start=(kc == k0), stop=(kc == k1 - 1))
                    self.rsq(ss, rg[gi], 128, 512, float((k1 - k0) * 128))
                for gi, (k0, k1) in enumerate(((0, 8), (8, 20), (20, 32))):
                    for kc in range(k0, k1):
                        P.op("dve", "scalar_tensor_tensor", [bigA.b, rg[gi].b], [bigA.b],
                             out=bigA[:, kc, :], in0=bigA[:, kc, :], scalar=self.pc(l, "gout", kc), in1=rg[gi][:, :],
                             op0=ALU.mult, op1=ALU.mult)
                for cb in range(8):
                    slab = wo_slab[cb % 2]
                    ex = [ffn_last_pe] if (ffn_last_pe is not None and cb < 2) else []
                    P.dma("sp", slab[:, :, :],
                          self.woutb[l][:, cb * 512:(cb + 1) * 512].rearrange("(kc p) c -> p kc c", p=128), [], [slab.b],
                          extra=ex)
                    for sub in range(4):
                        a = acc.next()
                        for kc in range(KC):
                            P.op("pe", "matmul", [bigA.b, slab.b], [a.b], out=a[:, :],
                                 lhsT=bigA[:, kc, sub * 128:(sub + 1) * 128], rhs=slab[:, kc, :], start=(kc == 0),
                                 stop=(kc == KC - 1))
                        P.op("dve", "tensor_tensor", [a.b, xacc[sub].b], [xacc[sub].b],
                             out=xacc[sub][:, cb * 512:(cb + 1) * 512], in0=a[:, :],
                             in1=xacc[sub][:, cb * 512:(cb + 1) * 512], op=ALU.add)
                wout_last_pe = P.last["pe"]
                for sub in range(4):
                    self.norm_transpose(xacc[sub], "gffn", l, sub, bigA, pool)
                first = {"a2T": True, "wup0": True, "wup1": True, "wdn0": True, "wdn1": True}
                nup = 0
                ndn = 0
                for g in range(8):
                    for i in range(16):
                        mb = g * 16 + i
                        w = wup[nup % 2]
                        ex = [wout_last_pe] if first[f"wup{nup % 2}"] else []
                        first[f"wup{nup % 2}"] = False
                        nup += 1
                        P.dma("sp", w[:, :, :], self.wupb[l, mb], [], [w.b], extra=ex)
                        a = acc.next()
                        for kc in range(KC):
                            P.op("pe", "matmul", [w.b, bigA.b], [a.b], out=a[:, :], lhsT=w[:, kc, :], rhs=bigA[:, kc, :],
                                 start=(kc == 0), stop=(kc == KC - 1))
                        r = rl.next()
                        P.op("act", "activation", [a.b], [r.b], out=r[:, :], in_=a[:, :], func=AF.Relu)
                        ex = [wout_last_pe] if first["a2T"] else []
                        first["a2T"] = False
                        P.opx("pool", "tensor_tensor", [r.b], [a2T.b], ex, out=a2T[:, i, :], in0=r[:, :], in1=r[:, :],
                              op=ALU.mult)
                    for cb in range(8):
                        w = wdn[ndn % 2]
                        ex = [wout_last_pe] if first[f"wdn{ndn % 2}"] else []
                        first[f"wdn{ndn % 2}"] = False
                        ndn += 1
                        P.dma("sp", w[:, :, :],
                              self.wdnb[l][g * 2048:(g + 1) * 2048, cb * 512:(cb + 1) * 512].rearrange(
                                  "(kc p) c -> p kc c", p=128), [], [w.b], extra=ex)
                        for sub in range(4):
                            a = acc.next()
                            for kc in range(16):
                                P.op("pe", "matmul", [a2T.b, w.b], [a.b], out=a[:, :],
                                     lhsT=a2T[:, kc, sub * 128:(sub + 1) * 128], rhs=w[:, kc, :], start=(kc == 0),
                                     stop=(kc == 15))
                            P.op("dve", "tensor_tensor", [a.b, xacc[sub].b], [xacc[sub].b],
                                 out=xacc[sub][:, cb * 512:(cb + 1) * 512], in0=a[:, :],
                                 in1=xacc[sub][:, cb * 512:(cb + 1) * 512], op=ALU.add)
                ffn_last_pe = P.last["pe"]
                for sub in range(4):
                    P.dma("pool", xdst[t0 + sub * 128:t0 + (sub + 1) * 128, :], xacc[sub][:, :], [xacc[sub].b], [],
                          out=final)
        P.barrier()

    def mla(self, s, l):
        P, c, sc = self.P, self.cL[s], self.S[s]
        L = c["L"]
        NKB, NQT = L // 128, L // 512
        scale = 192.0 ** -0.5
        with ExitStack() as es:
            krT = self.sb(es, "krT", [64, L], BF16)
            knT = Rot([self.sb(es, f"knT{i}", [128, L], BF16) for i in range(2)])
            V = Rot([self.sb(es, f"V{i}", [128, NKB, 128], BF16) for i in range(2)])
            qn = Rot([self.sb(es, f"qn{i}", [128, 512], BF16) for i in range(2)])
            qr = Rot([self.sb(es, f"qr{i}", [64, 512], BF16) for i in range(2)])
            Pt = Rot([self.sb(es, f"Pt{i}", [128, 512], BF16) for i in range(4)])
            S = Rot([self.ps(es, f"S{i}", [128, 512]) for i in range(4)])
            O = Rot([self.ps(es, f"O{i}", [128, 512]) for i in range(2)])
            Dn = Rot([self.ps(es, f"Dn{i}", [128, 512]) for i in range(2)])
            rec = Rot([self.sb(es, f"rec{i}", [128, 512], F32) for i in range(2)])
            ost = Rot([self.sb(es, f"ost{i}", [128, 512], BF16) for i in range(2)])
            P.dma("sp", krT[:, :], sc["krT"], [], [krT.b])
            for h in range(12):
                kn, v = knT.next(), V.next()
                P.dma("sp", kn[:, :], sc["knT"][h * 128:(h + 1) * 128, :], [], [kn.b])
                for k0 in range(0, NKB, 16):
                    P.dma("sp", v[:, k0:k0 + 16, :],
                          sc["vm"][k0 * 128:(k0 + 16) * 128, h * 128:(h + 1) * 128].rearrange("(kb p) c -> p kb c", p=128),
                          [], [v.b])
                for qt in range(NQT):
                    qsl = slice(qt * 512, (qt + 1) * 512)
                    q1, q2 = qn.next(), qr.next()
                    P.dma("sp", q1[:, :], sc["qnT"][h * 128:(h + 1) * 128, qsl], [], [q1.b])
                    P.dma("sp", q2[:, :], sc["qrT"][h * 64:(h + 1) * 64, qsl], [], [q2.b])
                    o, d = O.next(), Dn.next()

                    def issue_S(kb, kn=kn, q1=q1, q2=q2):
                        st = S.next()
                        ksl = slice(kb * 128, (kb + 1) * 128)
                        P.op("pe", "matmul", [kn.b, q1.b], [st.b], out=st[:, :], lhsT=kn[:, ksl], rhs=q1[:, :],
                             start=True, stop=False)
                        P.op("pe", "matmul", [krT.b, q2.b], [st.b], out=st[:, :], lhsT=krT[:, ksl], rhs=q2[:, :],
                             start=False, stop=True)
                        return st

                    sts = {0: issue_S(0)}
                    if NKB > 1:
                        sts[1] = issue_S(1)
                    for kb in range(NKB):
                        if kb + 2 < NKB:
                            sts[kb + 2] = issue_S(kb + 2)
                        st = sts.pop(kb)
                        p = Pt.next()
                        P.op("act", "activation", [st.b], [p.b], out=p[:, :], in_=st[:, :], func=AF.Exp, scale=scale)
                        P.op("pe", "matmul", [v.b, p.b], [o.b], out=o[:, :], lhsT=v[:, kb, :], rhs=p[:, :],
                             start=(kb == 0), stop=(kb == NKB - 1))
                        P.op("pe", "matmul", [self.ones.b, p.b], [d.b], out=d[:, :], lhsT=self.ones[:, :], rhs=p[:, :],
                             start=(kb == 0), stop=(kb == NKB - 1))
                    rc, os_ = rec.next(), ost.next()
                    P.op("dve", "reciprocal", [d.b], [rc.b], out=rc[:, :], in_=d[:, :])
                    P.op("dve", "tensor_tensor", [o.b, rc.b], [os_.b], out=os_[:, :], in0=o[:, :], in1=rc[:, :],
                         op=ALU.mult)
                    P.dma("pool", sc["ymixT"][1024 + h * 128:1024 + (h + 1) * 128, qsl], os_[:, :], [os_.b], [])
        P.barrier()

    def dil(self, s, l):
        P, c, sc = self.P, self.cL[s], self.S[s]
        L = c["L"]
        NW = L // 2048
        scale = 128.0 ** -0.5

        def ssl(start, count, step):
            return slice(start, start + (count - 1) * step + 1, step)

        with ExitStack() as es:
            Ob = [self.sb(es, f"Ob{g}", [128, 2048], F32) for g in range(3)]
            Db = [self.sb(es, f"Db{g}", [128, 2048], F32) for g in range(3)]
            dqs = Rot([self.sb(es, f"dq{i}", [128, 2048], BF16) for i in range(2)])
            dks = Rot([self.sb(es, f"dk{i}", [128, 6144], BF16) for i in range(2)])
            Pt = Rot([self.sb(es, f"Pd{i}", [128, 384], BF16) for i in range(3)])
            vts = Rot([self.sb(es, f"vt{i}", [128, 128], BF16) for i in range(9)])
            S = Rot([self.ps(es, f"Sd{i}", [128, 512]) for i in range(3)])
            O = Rot([self.ps(es, f"Od{i}", [128, 512]) for i in range(2)])
            Dn = Rot([self.ps(es, f"Dd{i}", [128, 512]) for i in range(2)])
            rec = self.sb(es, "recd", [128, 2048], F32)
            ost = Rot([self.sb(es, f"ostd{i}", [128, 2048], BF16) for i in range(2)])
            maskf = self.mask.t[:, :, :].rearrange("p k q -> p (k q)")
            for w in range(NW):
                w0 = w * 2048
                base = w0 - 2048
                for sidx in range(4):
                    for g, d in enumerate((1, 4, 16)):
                        h = g * 4 + sidx
                        n = L // d
                        nbw = 2048 // d // 128
                        dq, dk = dqs.next(), dks.next()
                        P.dma("sp", dq[:, :], sc["dqT"][h * 128:(h + 1) * 128, w0:w0 + 2048], [], [dq.b])
                        lo, hi = max(0, w0 - 128 * d), min(L, w0 + 2048 + 128 * d)
                        P.dma("sp", dk[:, lo - base:hi - base], sc["dkT"][h * 128:(h + 1) * 128, lo:hi], [], [dk.b])
                        def issue(r, b, d=d, n=n, h=h, dq=dq, dk=dk):
                            a0 = w0 // d + b * 128
                            qcols = ssl(b * 128 * d + r, 128, d)
                            kts = [kt for kt in (-1, 0, 1) if 0 <= a0 + 128 * kt < n]
                            st = S.next()
                            vt = []
                            for j, kt in enumerate(kts):
                                ak0 = a0 + 128 * kt
                                kcols = ssl(ak0 * d + r - base, 128, d)
                                P.op("pe", "matmul", [dk.b, dq.b], [st.b], out=st[:, j * 128:(j + 1) * 128],
                                     lhsT=dk[:, kcols], rhs=dq[:, qcols], start=True, stop=True)
                                v = vts.next()
                                P.dma("sp", v[:, :], sc["dv"][ssl(ak0 * d + r, 128, d), h * 128:(h + 1) * 128],
                                      [], [v.b])
                                vt.append(v)
                            return st, vt, kts, qcols

                        def finish(st, vt, kts, qcols, g=g):
                            nk = len(kts)
                            p = Pt.next()
                            P.op("act", "activation", [st.b], [p.b], out=p[:, 0:nk * 128], in_=st[:, 0:nk * 128],
                                 func=AF.Exp, scale=scale)
                            m0 = (kts[0] + 1) * 128
                            P.op("pool", "tensor_tensor", [p.b, self.mask.b], [p.b], out=p[:, 0:nk * 128],
                                 in0=p[:, 0:nk * 128], in1=maskf[:, m0:m0 + nk * 128], op=ALU.mult)
                            o, dn = O.next(), Dn.next()
                            for j in range(nk):
                                P.op("pe", "matmul", [vt[j].b, p.b], [o.b], out=o[:, 0:128], lhsT=vt[j][:, :],
                                     rhs=p[:, j * 128:(j + 1) * 128], start=(j == 0), stop=(j == nk - 1))
                            for j in range(nk):
                                P.op("pe", "matmul", [self.ones.b, p.b], [dn.b], out=dn[:, 0:128],
                                     lhsT=self.ones[:, :], rhs=p[:, j * 128:(j + 1) * 128], start=(j == 0),
                                     stop=(j == nk - 1))
                            P.op("act", "activation", [o.b], [Ob[g].b], out=Ob[g][:, qcols], in_=o[:, 0:128],
                                 func=AF.Copy)
                            P.op("dve", "tensor_copy", [dn.b], [Db[g].b], out=Db[g][:, qcols], in_=dn[:, 0:128])

                        units = [(r, b) for r in range(d) for b in range(nbw)]
                        nxt = issue(*units[0])
                        for ui in range(len(units)):
                            cur = nxt
                            if ui + 1 < len(units):
                                nxt = issue(*units[ui + 1])
                            finish(*cur)
                    P.op("dve", "tensor_tensor", [Db[0].b, Db[1].b], [rec.b], out=rec[:, :], in0=Db[0][:, :],
                         in1=Db[1][:, :], op=ALU.add)
                    P.op("dve", "tensor_tensor", [rec.b, Db[2].b], [rec.b], out=rec[:, :], in0=rec[:, :],
                         in1=Db[2][:, :], op=ALU.add)
                    P.op("dve", "reciprocal", [rec.b], [rec.b], out=rec[:, :], in_=rec[:, :])
                    for g in range(3):
                        h = g * 4 + sidx
                        o2 = ost.next()
                        P.op("pool" if g == 1 else "dve", "tensor_tensor", [Ob[g].b, rec.b], [o2.b], out=o2[:, :],
                             in0=Ob[g][:, :], in1=rec[:, :], op=ALU.mult)
                        P.dma("pool", sc["ymixT"][2560 + h * 128:2560 + (h + 1) * 128, w0:w0 + 2048], o2[:, :],
                              [o2.b], [])
        P.barrier()

    def hyena(self, s, l):
        self.hy_conv(s, l)
        self.hy_filter(s, l)
        self.hy_fwd(s, l)
        self.hy_inv(s, l)

    def hy_conv(self, s, l):
        P, c, sc = self.P, self.cL[s], self.S[s]
        L = c["L"]
        PC = self.PCOL
        zsrc = sc["zhyT"].rearrange("(part c) t -> c part t", c=HY)
        with ExitStack() as es:
            zts = Rot([self.sb(es, f"zt{i}", [128, 3, 514], BF16) for i in range(2)])
            zc = [Rot([self.sb(es, f"zc{k}_{i}", [128, 512], F32) for i in range(2)]) for k in range(3)]
            x0b = Rot([self.sb(es, f"x0b{i}", [128, 512], BF16) for i in range(2)])
            ub = Rot([self.sb(es, f"ub{i}", [128, 512], BF16) for i in range(2)])
            tp = Rot([self.ps(es, f"tph{i}", [128, 4, 128], BF16) for i in range(2)])
            us = Rot([self.sb(es, f"us{i}", [128, 4, 128], BF16) for i in range(2)])
            for cb in range(8):
                for ti in range(L // 512):
                    t0 = ti * 512
                    lo, hi = max(0, t0 - 1), min(L, t0 + 513)
                    zt = zts.next()
                    if t0 == 0:
                        P.op("pool", "memset", [], [zt.b], ap=zt[:, :, 0:1], constant=0.0)
                    if t0 + 512 == L:
                        P.op("pool", "memset", [], [zt.b], ap=zt[:, :, 513:514], constant=0.0)
                    P.dma("sp", zt[:, :, lo - (t0 - 1):hi - (t0 - 1)], zsrc[cb * 128:(cb + 1) * 128, :, lo:hi], [], [zt.b])
                    outs = []
                    for part in range(3):
                        eng = "dve"
                        z = zc[part].next()
                        wcol = lambda tap: self.pm[l][:, PC["cw"] + (part * 8 + cb) * 3 + tap:PC["cw"] + (part * 8 + cb) * 3 + tap + 1]
                        bcol = self.pm[l][:, PC["cbias"] + part * 8 + cb:PC["cbias"] + part * 8 + cb + 1]
                        P.op(eng, "tensor_scalar", [zt.b], [z.b], out=z[:, :], in0=zt[:, part, 0:512], scalar1=wcol(0),
                             scalar2=bcol, op0=ALU.mult, op1=ALU.add)
                        P.op(eng, "scalar_tensor_tensor", [zt.b, z.b], [z.b], out=z[:, :], in0=zt[:, part, 1:513],
                             scalar=wcol(1), in1=z[:, :], op0=ALU.mult, op1=ALU.add)
                        if part == 0:
                            xb = x0b.next()
                            P.op(eng, "scalar_tensor_tensor", [zt.b, z.b], [xb.b], out=xb[:, :], in0=zt[:, part, 2:514],
                                 scalar=wcol(2), in1=z[:, :], op0=ALU.mult, op1=ALU.add)
                            P.dma("pool", sc["x0cT"][cb * 128:(cb + 1) * 128, t0:t0 + 512], xb[:, :], [xb.b], [])
                        else:
                            P.op(eng, "scalar_tensor_tensor", [zt.b, z.b], [z.b], out=z[:, :], in0=zt[:, part, 2:514],
                                 scalar=wcol(2), in1=z[:, :], op0=ALU.mult, op1=ALU.add)
                        outs.append(z)
                    u = ub.next()
                    P.op("dve", "tensor_tensor", [outs[1].b, outs[2].b], [u.b], out=u[:, :], in0=outs[1][:, :],
                         in1=outs[2][:, :], op=ALU.mult)
                    P.dma("pool", sc["uT"][cb * 128:(cb + 1) * 128, t0:t0 + 512], u[:, :], [u.b], [])
                    t = tp.next()
                    for sub in range(4):
                        P.op("pe", "transpose", [u.b, self.ident.b], [t.b], out=t[:, sub, :],
                             in_=u[:, sub * 128:(sub + 1) * 128], identity=self.ident[:, :])
                    u2 = us.next()
                    P.op("act", "activation", [t.b], [u2.b], out=u2[:, :, :], in_=t[:, :, :], func=AF.Copy)
                    P.dma("pool", sc["utm"][t0:t0 + 512, cb * 128:(cb + 1) * 128].rearrange("(s p) c -> p s c", p=128),
                          u2[:, :, :], [u2.b], [])
        P.barrier()

    def hy_filter(self, s, l):
        P, c, sc = self.P, self.cL[s], self.S[s]
        L = c["L"]
        PI = math.pi
        with ExitStack() as es:
            w1 = self.sb(es, "fw1", [33, 64], F32)
            w2 = self.sb(es, "fw2", [64, 64], F32)
            w3 = self.sb(es, "fw3", [64, 64], F32)
            w4 = self.sb(es, "fw4", [64, 2048], F32)
            b4 = self.sb(es, "fb4", [128, 2048], F32)
            P.dma("sp", w1[:, :], self.fw1[l], [], [w1.b])
            P.dma("sp", w2[:, :], self.fw2[l], [], [w2.b])
            P.dma("sp", w3[:, :], self.fw3[l], [], [w3.b])
            P.dma("sp", w4[:, :], self.fw4[l], [], [w4.b])
            P.dma("sp", b4[:, :], self.fb4[l].partition_broadcast(128), [], [b4.b])
            zf = Rot([self.sb(es, f"zf{i}", [33, 512], F32) for i in range(2)])
            hs = Rot([self.sb(es, f"hs{i}", [64, 512], F32) for i in range(3)])
            cc = Rot([self.sb(es, f"cc{i}", [64, 512], F32) for i in range(4)])
            pz = Rot([self.ps(es, f"pz{i}", [128, 512]) for i in range(2)])
            p4 = Rot([self.ps(es, f"p4{i}", [128, 512]) for i in range(4)])
            hh = Rot([self.sb(es, f"hh{i}", [128, 2048], F32) for i in range(2)])
            dcf = Rot([self.sb(es, f"dcf{i}", [128, HY], F32) for i in range(2)])
            dcb = Rot([self.sb(es, f"dcb{i}", [128, HY], F32) for i in range(2)])
            Ao = Rot([self.sb(es, f"Ao{i}", [128, HY], BF16) for i in range(2)])
            Bo = Rot([self.sb(es, f"Bo{i}", [128, HY], BF16) for i in range(2)])
            for ti in range(L // 512):
                t0 = ti * 512
                z = zf.next()
                P.dma("sp", z[:, :], c["zfT"][:, t0:t0 + 512], [], [z.b])
                cur, curK, wts = z, 33, (w1, w2, w3)
                for li in range(3):
                    pp = pz.next()
                    P.op("pe", "matmul", [wts[li].b, cur.b], [pp.b], out=pp[0:64, :], lhsT=wts[li][0:curK, :],
                         rhs=cur[0:curK, :], start=True, stop=True)
                    a = hs.next()
                    P.op("dve", "tensor_scalar", [pp.b], [a.b], out=a[:, :], in0=pp[0:64, :],
                         scalar1=self.pc(l, ("b1", "b2", "b3")[li], rows=64), scalar2=self.pc(l, "freq", rows=64),
                         op0=ALU.add, op1=ALU.mult)
                    for thr in (PI, 3 * PI):
                        c1, c2 = cc.next(), cc.next()
                        P.op("dve", "tensor_scalar", [a.b], [c1.b], out=c1[:, :], in0=a[:, :], scalar1=thr,
                             scalar2=-2 * PI, op0=ALU.is_gt, op1=ALU.mult)
                        P.op("pool", "tensor_scalar", [a.b], [c2.b], out=c2[:, :], in0=a[:, :], scalar1=-thr,
                             scalar2=2 * PI, op0=ALU.is_lt, op1=ALU.mult)
                        P.op("dve", "tensor_tensor", [c1.b, c2.b], [c1.b], out=c1[:, :], in0=c1[:, :], in1=c2[:, :],
                             op=ALU.add)
                        if thr == PI:
                            cacc = c1
                        else:
                            P.op("dve", "tensor_tensor", [c1.b, cacc.b], [cacc.b], out=cacc[:, :], in0=c1[:, :],
                                 in1=cacc[:, :], op=ALU.add)
                    P.op("dve", "tensor_tensor", [a.b, cacc.b], [a.b], out=a[:, :], in0=a[:, :], in1=cacc[:, :], op=ALU.add)
                    P.op("act", "activation", [a.b], [a.b], out=a[:, :], in_=a[:, :], func=AF.Sin)
                    cur, curK = a, 64
                for sub in range(4):
                    h4 = hh.next()
                    for cbk in range(4):
                        pp = p4.next()
                        P.op("pe", "matmul", [cur.b, w4.b], [pp.b], out=pp[:, :], lhsT=cur[0:64, sub * 128:(sub + 1) * 128],
                             rhs=w4[0:64, cbk * 512:(cbk + 1) * 512], start=True, stop=True)
                        P.op("dve", "tensor_tensor", [pp.b, b4.b], [h4.b], out=h4[:, cbk * 512:(cbk + 1) * 512], in0=pp[:, :],
                             in1=b4[:, cbk * 512:(cbk + 1) * 512], op=ALU.add)
                    df, db = dcf.next(), dcb.next()
                    r0 = t0 + sub * 128
                    P.dma("sp", df[:, :], c["decf"][r0:r0 + 128, :], [], [df.b])
                    P.dma("sp", db[:, :], c["decb"][r0:r0 + 128, :], [], [db.b])
                    P.op("dve", "tensor_tensor", [h4.b, df.b], [h4.b], out=h4[:, 0:HY], in0=h4[:, 0:HY], in1=df[:, :],
                         op=ALU.mult)
                    P.op("pool", "tensor_tensor", [h4.b, db.b], [h4.b], out=h4[:, HY:2 * HY], in0=h4[:, HY:2 * HY],
                         in1=db[:, :], op=ALU.mult)
                    A, Bm = Ao.next(), Bo.next()
                    P.op("dve", "tensor_tensor", [h4.b], [A.b], out=A[:, :], in0=h4[:, 0:HY], in1=h4[:, HY:2 * HY],
                         op=ALU.add)
                    P.op("pool", "tensor_tensor", [h4.b], [Bm.b], out=Bm[:, :], in0=h4[:, 0:HY], in1=h4[:, HY:2 * HY],
                         op=ALU.subtract)
                    P.dma("pool", sc["Atm"][r0:r0 + 128, :], A[:, :], [A.b], [])
                    P.dma("pool", sc["Btm"][r0:r0 + 128, :], Bm[:, :], [Bm.b], [])
        P.barrier()

    def hy_fwd(self, s, l):
        P, c, sc = self.P, self.cL[s], self.S[s]
        L = c["L"]
        NB = L // 128
        with ExitStack() as es:
            ops3 = [self.sb(es, f"opd{i}", [128, NB, 256], BF16) for i in range(3)]
            Cb = Rot([self.sb(es, f"Cb{i}", [128, NB, 128], BF16) for i in range(2)])
            Sb = Rot([self.sb(es, f"Sb{i}", [128, NB, 128], BF16) for i in range(2)])
            banks = [Rot([self.ps(es, f"F{k}{i}", [128, 512]) for i in range(2)]) for k in range(4)]
            ks = [Rot([self.sb(es, f"ks{k}{i}", [128, 256], F32) for i in range(2)]) for k in range(2)]
            tt = Rot([self.sb(es, f"tt{i}", [128, 256], F32) for i in range(4)])
            yo = [Rot([self.sb(es, f"yo{k}{i}", [128, 2, 128], BF16) for i in range(2)]) for k in range(2)]
            srcs = [sc["utm"], sc["Atm"], sc["Btm"]]
            for cq in range(4):
                for k3 in range(3):
                    for k0 in range(0, NB, 16):
                        P.dma("sp", ops3[k3][:, k0:k0 + 16, :],
                              srcs[k3][k0 * 128:(k0 + 16) * 128, cq * 256:(cq + 1) * 256].rearrange("(tc p) c -> p tc c", p=128),
                              [], [ops3[k3].b])
                u, A, Bm = ops3
                for fb in range(NB):
                    cblk, sblk = Cb.next(), Sb.next()
                    P.dma("sp", cblk[:, :, :], c["Cf"][fb], [], [cblk.b])
                    P.dma("sp", sblk[:, :, :], c["Sf"][fb], [], [sblk.b])
                    Ur, Kr, Ui, Ki = [banks[k].next() for k in range(4)]
                    for tc in range(NB):
                        st, sp_ = (tc == 0), (tc == NB - 1)
                        P.op("pe", "matmul", [cblk.b, u.b], [Ur.b], out=Ur[:, 0:256], lhsT=cblk[:, tc, :], rhs=u[:, tc, :],
                             start=st, stop=sp_)
                        P.op("pe", "matmul", [cblk.b, A.b], [Kr.b], out=Kr[:, 0:256], lhsT=cblk[:, tc, :], rhs=A[:, tc, :],
                             start=st, stop=sp_)
                        P.op("pe", "matmul", [sblk.b, u.b], [Ui.b], out=Ui[:, 0:256], lhsT=sblk[:, tc, :], rhs=u[:, tc, :],
                             start=st, stop=sp_)
                        P.op("pe", "matmul", [sblk.b, Bm.b], [Ki.b], out=Ki[:, 0:256], lhsT=sblk[:, tc, :], rhs=Bm[:, tc, :],
                             start=st, stop=sp_)
                    krs, kis = ks[0].next(), ks[1].next()
                    P.op("act", "activation", [Kr.b], [krs.b], out=krs[:, :], in_=Kr[:, 0:256], func=AF.Copy)
                    P.op("act", "activation", [Ki.b], [kis.b], out=kis[:, :], in_=Ki[:, 0:256], func=AF.Copy)
                    t1, t2, t3, t4 = tt.next(), tt.next(), tt.next(), tt.next()
                    P.op("dve", "tensor_tensor", [Ur.b, krs.b], [t1.b], out=t1[:, :], in0=Ur[:, 0:256], in1=krs[:, :], op=ALU.mult)
                    P.op("dve", "tensor_tensor", [Ui.b, kis.b], [t2.b], out=t2[:, :], in0=Ui[:, 0:256], in1=kis[:, :], op=ALU.mult)
                    P.op("dve", "tensor_tensor", [Ui.b, krs.b], [t3.b], out=t3[:, :], in0=Ui[:, 0:256], in1=krs[:, :], op=ALU.mult)
                    P.op("dve", "tensor_tensor", [Ur.b, kis.b], [t4.b], out=t4[:, :], in0=Ur[:, 0:256], in1=kis[:, :], op=ALU.mult)
                    yr, yi = yo[0].next(), yo[1].next()
                    P.op("pool", "tensor_tensor", [t1.b, t2.b], [yr.b], out=yr[:, :, :].rearrange("p b c -> p (b c)"),
                         in0=t1[:, :], in1=t2[:, :], op=ALU.subtract)
                    P.op("pool", "tensor_tensor", [t3.b, t4.b], [yi.b], out=yi[:, :, :].rearrange("p b c -> p (b c)"),
                         in0=t3[:, :], in1=t4[:, :], op=ALU.add)
                    P.dma("pool", sc["Yrb"][cq * 2:cq * 2 + 2, :, fb, :].rearrange("b p c -> p b c"), yr[:, :, :], [yr.b], [])
                    P.dma("pool", sc["Yib"][cq * 2:cq * 2 + 2, :, fb, :].rearrange("b p c -> p b c"), yi[:, :, :], [yi.b], [])
        P.barrier()

    def hy_inv(self, s, l):
        P, c, sc = self.P, self.cL[s], self.S[s]
        L = c["L"]
        NB = L // 128
        with ExitStack() as es:
            Ct = Rot([self.sb(es, "Cit", [128, NB, 256], BF16)])
            St = Rot([self.sb(es, "Sit", [128, NB, 256], BF16)])
            Yr = Rot([self.sb(es, f"Yr{i}", [128, NB, 128], BF16) for i in range(2)])
            Yi = Rot([self.sb(es, f"Yi{i}", [128, NB, 128], BF16) for i in range(2)])
            yp = Rot([self.ps(es, f"yp{i}", [128, 512]) for i in range(3)])
            ut = Rot([self.sb(es, f"ut{i}", [128, 256], BF16) for i in range(2)])
            x0t = Rot([self.sb(es, f"x0t{i}", [128, 256], BF16) for i in range(2)])
            y2 = Rot([self.sb(es, f"y2{i}", [128, 256], F32) for i in range(2)])
            y3 = Rot([self.sb(es, f"y3{i}", [128, 256], BF16) for i in range(2)])
            for tt in range(L // 256):
                tsl = slice(tt * 256, (tt + 1) * 256)
                ct, st_ = Ct.next(), St.next()
                P.dma("sp", ct[:, :, :], c["Ci"][tt], [], [ct.b])
                P.dma("sp", st_[:, :, :], c["Si"][tt], [], [st_.b])
                for cb in range(8):
                    yr, yi = Yr.next(), Yi.next()
                    P.dma("sp", yr[:, :, :], sc["Yrb"][cb], [], [yr.b])
                    P.dma("sp", yi[:, :, :], sc["Yib"][cb], [], [yi.b])
                    y = yp.next()
                    for fc in range(NB):
                        P.op("pe", "matmul", [yr.b, ct.b], [y.b], out=y[:, 0:256], lhsT=yr[:, fc, :], rhs=ct[:, fc, :],
                             start=(fc == 0), stop=False)
                        P.op("pe", "matmul", [yi.b, st_.b], [y.b], out=y[:, 0:256], lhsT=yi[:, fc, :], rhs=st_[:, fc, :],
                             start=False, stop=(fc == NB - 1))
                    u, x0 = ut.next(), x0t.next()
                    P.dma("sp", u[:, :], sc["uT"][cb * 128:(cb + 1) * 128, tsl], [], [u.b])
                    P.dma("sp", x0[:, :], sc["x0cT"][cb * 128:(cb + 1) * 128, tsl], [], [x0.b])
                    a, b3 = y2.next(), y3.next()
                    P.op("dve", "scalar_tensor_tensor", [u.b, y.b], [a.b], out=a[:, :], in0=u[:, :],
                         scalar=self.pc(l, "skip", cb), in1=y[:, 0:256], op0=ALU.mult, op1=ALU.add)
                    P.op("pool", "tensor_tensor", [a.b, x0.b], [b3.b], out=b3[:, :], in0=a[:, :], in1=x0[:, :], op=ALU.mult)
                    P.dma("pool", sc["ymixT"][cb * 128:(cb + 1) * 128, tsl], b3[:, :], [b3.b], [])
        P.barrier()


def _bf(a):
    return np.ascontiguousarray(a.astype(ml_dtypes.bfloat16))


def _consts_for_L(L):
    f32 = np.float32
    out = {}
    for nm, dim in (("m", 64), ("d", 128)):
        inv = (1.0 / (f32(10000.0) ** (np.arange(0, dim, 2, dtype=f32) / f32(dim)))).astype(f32)
        ang = (np.arange(L, dtype=f32)[:, None] * inv[None, :]).astype(f32)
        cos, sin = np.cos(ang).astype(f32), np.sin(ang).astype(f32)
        out["cos" + nm] = np.ascontiguousarray(np.concatenate([cos, cos], 1).T)
        out["sin" + nm] = np.ascontiguousarray(np.concatenate([sin, sin], 1).T)
    t = np.linspace(0.0, 1.0, L, dtype=f32)[:, None]
    w = (f32(2.0 * math.pi) * np.arange(L, dtype=f32)[:, None] / f32(L)).astype(f32)
    f = np.linspace(1e-4, 15, 16, dtype=f32)[None, :]
    z = np.concatenate([t, np.cos(f * w), -np.sin(f * w)], axis=-1).astype(f32)
    out["zfT"] = np.ascontiguousarray(z.T)
    max_decay = math.log(1e-2) / 0.3
    min_decay = math.log(1e-2) / 1.5
    deltas = np.abs(np.linspace(min_decay, max_decay, HY, dtype=f32))
    dec = np.exp(-t * deltas[None, :]).astype(f32)
    out["decf"] = dec
    decb = dec.copy()
    decb[0] = 0.0
    out["decb"] = decb
    nb = L // 128
    ff = np.arange(L, dtype=np.int64)
    k = ((2 * ff[None, :] + 1) * ff[:, None]) % (4 * L)
    ang = k.astype(np.float64) * (math.pi / (2 * L))
    C = np.cos(ang).astype(f32)
    S = (-np.sin(ang)).astype(f32)
    del ang, k

    def blk_f(M):
        return _bf(M.reshape(nb, 128, nb, 128).transpose(2, 1, 0, 3))

    def blk_i(M):
        Mi = (M.T / f32(L)).astype(f32)
        return _bf(Mi.reshape(nb, 128, L // 256, 256).transpose(2, 1, 0, 3))

    out["Cf"], out["Sf"] = blk_f(C), blk_f(S)
    out["Ci"], out["Si"] = blk_i(C), blk_i(S)
    return out


def _shared_consts():
    c = {}
    c["c_ident"] = _bf(np.eye(128, dtype=np.float32))
    c["c_ones"] = _bf(np.ones((128, 128), np.float32))
    for nm, dim in (("c_r64", 64), ("c_r128", 128)):
        h = dim // 2
        R = np.zeros((dim, dim), np.float32)
        for m in range(dim):
            if m < h:
                R[m + h, m] = -1.0
            else:
                R[m - h, m] = 1.0
        c[nm] = _bf(R)
    p = np.arange(128)[:, None]
    q = np.arange(128)[None, :]
    mk = np.stack([(p - q >= 64), (np.abs(q - p) <= 64), (q - p >= 64)], axis=1).astype(np.float32)
    c["c_mask"] = _bf(mk)
    return c


def _pack_params(inp, depth):
    PC = Builder.PCOL
    pm = np.zeros((depth, 128, Builder.NPCOL), np.float32)

    def col(v):
        return np.asarray(v, np.float32).reshape(-1, 128).T

    for l in range(depth):
        pm[l, :, PC["gmix"]:PC["gmix"] + 32] = col(inp["norm_mix"][l])
        pm[l, :, PC["gffn"]:PC["gffn"] + 32] = col(inp["norm_ffn"][l])
        pm[l, :, PC["gout"]:PC["gout"] + 32] = col(inp["out_norm"][l])
        pm[l, :, PC["gqa"]:PC["gqa"] + 12] = col(inp["mla_q_a_norm"][l])
        pm[l, :, PC["gkva"]:PC["gkva"] + 4] = col(inp["mla_kv_a_norm"][l])
        pm[l, :64, PC["gknr"]] = inp["mla_kn_rope"][l]
        pm[l, :, PC["gqnn"]] = inp["mla_qn_nope"][l]
        pm[l, :64, PC["gqnr"]] = inp["mla_qn_rope"][l]
        pm[l, :, PC["gknn"]] = inp["mla_kn_nope"][l]
        pm[l, :, PC["gdq"]] = inp["dil_q_norm"][l]
        pm[l, :, PC["gdk"]] = inp["dil_k_norm"][l]
        cw = np.asarray(inp["hy_conv_w"][l], np.float32)
        cwr = cw.reshape(3, 3, 8, 128)
        pm[l, :, PC["cw"]:PC["cw"] + 72] = cwr.transpose(3, 1, 2, 0).reshape(128, 72)
        cb = np.asarray(inp["hy_conv_b"][l], np.float32).reshape(3, 8, 128)
        pm[l, :, PC["cbias"]:PC["cbias"] + 24] = cb.transpose(2, 0, 1).reshape(128, 24)
        pm[l, :, PC["skip"]:PC["skip"] + 8] = col(inp["hy_skip"][l])
        pm[l, :64, PC["b1"]] = inp["hy_f_b1"][l]
        pm[l, :64, PC["b2"]] = inp["hy_f_b2"][l]
        pm[l, :64, PC["b3"]] = inp["hy_f_b3"][l]
        pm[l, :64, PC["freq"]] = inp["hy_f_freq"][l]
    return pm


_CONST_CACHE = {}


def host_inputs(inp, seqs=("p", "s"), depth=DEPTH):
    a = lambda k: np.ascontiguousarray(np.asarray(inp[k], np.float32)[:depth])
    common = dict(
        w_in=a("w_in"), w_out=a("w_out"), w_up=a("w_up"), w_down=a("w_down"), w_qb=a("mla_w_q_b"),
        w_kvb=a("mla_w_kv_b"), parm=_pack_params(inp, depth), fw1=a("hy_f_w1"), fw2=a("hy_f_w2"),
        fw3=a("hy_f_w3"), fw4=a("hy_f_w4"), fb4=np.ascontiguousarray(a("hy_f_b4")[:, None, :]),
    )
    common.update(_shared_consts())
    for s in seqs:
        L = LP if s == "p" else LS
        if L not in _CONST_CACHE:
            _CONST_CACHE[L] = _consts_for_L(L)
        for k, v in _CONST_CACHE[L].items():
            common[f"{k}_{s}"] = v
    return common


ALL_PHASES = ("CAST", "A1", "A2", "HY", "MLA", "DIL", "C")


def build(taps=(), phases=ALL_PHASES, seqs=("p", "s"), depth=DEPTH, layers=None):
    B = Builder(taps, None, seqs, depth)
    B.declare()
    B.load_consts()
    B.P.barrier()
    if "CAST" in phases:
        B.cast_weights()
    for l in (range(depth) if layers is None else layers):
        for s in seqs:
            xsrc = B.xin[s] if l == 0 else B.S[s]["xmid"]
            xdst = B.yout[s] if l == depth - 1 else B.S[s]["xmid"]
            if "A1" in phases:
                B.phase_A1(s, l, xsrc)
            if "A2" in phases:
                B.phase_A2(s, l)
            if "HY" in phases:
                B.hyena(s, l)
            if "MLA" in phases:
                B.mla(s, l)
            if "DIL" in phases:
                B.dil(s, l)
            if "C" in phases:
                B.phase_C(s, l, xsrc, xdst, final=(l == depth - 1))
    B.P.barrier()
    B.P.emit()
    return B


def kernel(**inputs):
    B = build()
    hi = host_inputs(inputs)
    xs = np.ascontiguousarray(np.asarray(inputs["x_sample"], np.float32)[0])
    maps = []
    for i in range(NCORES):
        m = dict(hi)
        m["xp"] = np.ascontiguousarray(np.asarray(inputs["x_prompt"], np.float32)[i])
        m["xs"] = xs
        maps.append(m)
    res = run_bass_kernel_spmd(B.nc, maps, core_ids=list(range(NCORES)))
    yp = np.stack([np.asarray(res.results[i]["yp"], np.float32) for i in range(NCORES)], 0)
    ys = np.asarray(res.results[0]["ys"], np.float32)[None]
    return yp, ys
```
